# Optimizing a Trainium2 kernel written in Bass

```python
import jax, jax.numpy as jnp
from jax import lax
import numpy as np

D_MODEL = 2048
BATCH = 4
SEQ = 8192
DEPTH = 2
DEC_BATCH = 4
DEC_SEQ = 2048
PAST_LEN = 128

A_HEADS = 6
A_HEAD_DIM = 128
A_WIDTH = 768
DILATED_CONFIGS = ((128, 1), (512, 4), (2048, 16))
B_HEADS = 6
Q_LORA = 512
KV_LORA = 512
QK_NOPE = 128
QK_ROPE = 64
V_HEAD = 128
B_WIDTH = 768
ROPE_THETA = 10000.0
MLA_QBLOCK = 128
C_WIDTH = 512
C_BLOCKS = 8
C_BLOCK_W = 64
CONV_WIDTH = 4
RG_C = 8.0
MIX_WIDTH = 2048
IN_WIDTH = 4416
IN_CUTS = (2304, 2816, 3328, 3392, 3904)
X_HEADS = 4
X_HEAD_DIM = 128
X_WIDTH = 512
N_MEM = 256
D_FF = 5632
EPS = 1e-6
NEG_INF = -1e30

kernel_name = "hybrid_bidir_dilated_mla_rglru_encoder"


def _rmsnorm(x, g):
    xf = x.astype(jnp.float32)
    y = xf * lax.rsqrt(jnp.mean(xf * xf, axis=-1, keepdims=True) + EPS)
    return (y * g.astype(jnp.float32)).astype(x.dtype)


def _swiglu(h, w_gate, w_up, w_down):
    return (jax.nn.silu(h @ w_gate) * (h @ w_up)) @ w_down


def _alibi_slopes(n):
    return 2.0 ** (-8.0 * jnp.arange(1, n + 1, dtype=jnp.float32) / n)


def _band_attention(q, k, v, slopes, dil, half):
    N, L, H, dh = q.shape
    blk = half
    nb = -(-L // blk)
    lp = nb * blk
    qb = jnp.pad(q, ((0, 0), (0, lp - L), (0, 0), (0, 0))).reshape(N, nb, blk, H, dh)
    kv_pad = ((0, 0), (half, lp - L + half), (0, 0), (0, 0))
    kb = jnp.pad(k, kv_pad).reshape(N, nb + 2, blk, H, dh)
    vb = jnp.pad(v, kv_pad).reshape(N, nb + 2, blk, H, dh)
    kw = jnp.concatenate([kb[:, :-2], kb[:, 1:-1], kb[:, 2:]], axis=2)
    vw = jnp.concatenate([vb[:, :-2], vb[:, 1:-1], vb[:, 2:]], axis=2)
    s = jnp.einsum('nbqhd,nbkhd->nbhqk', qb, kw, preferred_element_type=jnp.float32) * (dh ** -0.5)
    qpos = jnp.arange(lp).reshape(nb, blk)
    kpos = jnp.arange(nb)[:, None] * blk - half + jnp.arange(3 * blk)[None, :]
    rel = jnp.abs(qpos[:, :, None] - kpos[:, None, :])
    valid = (rel <= half) & (kpos[:, None, :] >= 0) & (kpos[:, None, :] < L)
    bias = -slopes[None, :, None, None] * (dil * rel).astype(jnp.float32)[:, None]
    s = jnp.where(valid[:, None], s + bias, NEG_INF)
    lse = jax.nn.logsumexp(s, axis=-1)
    p = jnp.exp(s - lse[..., None])
    o = jnp.einsum('nbhqk,nbkhd->nbqhd', p.astype(v.dtype), vw)
    o = o.reshape(N, lp, H, dh)[:, :L]
    lse = lse.transpose(0, 1, 3, 2).reshape(N, lp, H)[:, :L]
    return o, lse


def _dilated_mixture(q, k, v, slopes):
    B, S, H, dh = q.shape
    outs, lses = [], []
    for window, dil in DILATED_CONFIGS:
        half = (window // 2) // dil
        n_cls = S // dil
        def to_cls(t):
            return t.reshape(B, n_cls, dil, H, dh).transpose(0, 2, 1, 3, 4).reshape(B * dil, n_cls, H, dh)
        o, lse = _band_attention(to_cls(q), to_cls(k), to_cls(v), slopes, dil, half)
        outs.append(o.reshape(B, dil, n_cls, H, dh).transpose(0, 2, 1, 3, 4).reshape(B, S, H, dh))
        lses.append(lse.reshape(B, dil, n_cls, H).transpose(0, 2, 1, 3).reshape(B, S, H))
    w = jax.nn.softmax(jnp.stack(lses), axis=0)
    o = jnp.sum(w[..., None] * jnp.stack(outs).astype(jnp.float32), axis=0)
    return o.astype(q.dtype).reshape(B, S, H * dh)


def _rope_cos_sin(S):
    inv = ROPE_THETA ** (-jnp.arange(0, QK_ROPE, 2, dtype=jnp.float32) / QK_ROPE)
    ang = jnp.arange(S, dtype=jnp.float32)[:, None] * inv[None, :]
    return jnp.cos(ang), jnp.sin(ang)


def _apply_rope(x, cos, sin):
    xf = x.astype(jnp.float32)
    x1, x2 = jnp.split(xf, 2, axis=-1)
    return jnp.concatenate([x1 * cos - x2 * sin, x2 * cos + x1 * sin], axis=-1).astype(x.dtype)


def _mla(c_q, c_kv, k_rope, g_q_lat, w_q_up, g_kv_lat, w_kv_up):
    B, S, _ = c_q.shape
    q = (_rmsnorm(c_q, g_q_lat) @ w_q_up).reshape(B, S, B_HEADS, QK_NOPE + QK_ROPE)
    kv = (_rmsnorm(c_kv, g_kv_lat) @ w_kv_up).reshape(B, S, B_HEADS, QK_NOPE + V_HEAD)
    q_nope, q_rope = q[..., :QK_NOPE], q[..., QK_NOPE:]
    k_nope, v = kv[..., :QK_NOPE], kv[..., QK_NOPE:]
    cos, sin = _rope_cos_sin(S)
    q_rope = _apply_rope(q_rope, cos[:, None, :], sin[:, None, :])
    k_rope = _apply_rope(k_rope, cos, sin)
    scale = (QK_NOPE + QK_ROPE) ** -0.5
    nq = S // MLA_QBLOCK

    def block(args):
        qn, qr = args
        s = (jnp.einsum('bqhd,bkhd->bhqk', qn, k_nope, preferred_element_type=jnp.float32)
             + jnp.einsum('bqhr,bkr->bhqk', qr, k_rope, preferred_element_type=jnp.float32))
        p = jax.nn.softmax(s * scale, axis=-1)
        return jnp.einsum('bhqk,bkhd->bqhd', p.astype(v.dtype), v)

    qn_b = q_nope.reshape(B, nq, MLA_QBLOCK, B_HEADS, QK_NOPE).swapaxes(0, 1)
    qr_b = q_rope.reshape(B, nq, MLA_QBLOCK, B_HEADS, QK_ROPE).swapaxes(0, 1)
    o = lax.map(block, (qn_b, qr_b))
    return o.swapaxes(0, 1).reshape(B, S, B_WIDTH)


def _linear_scan(a, b, reverse):
    if reverse:
        a, b = jnp.flip(a, 1), jnp.flip(b, 1)

    def comb(l, r):
        return l[0] * r[0], r[0] * l[1] + r[1]

    h = lax.associative_scan(comb, (a, b), axis=1)[1]
    return jnp.flip(h, 1) if reverse else h


def _rglru_branch(u, gate, conv_w, conv_b, w_r, b_r, w_i, b_i, lam):
    B, S, C = u.shape
    u = lax.conv_general_dilated(
        u, conv_w[:, None, :], window_strides=(1,),
        padding=[(CONV_WIDTH // 2, CONV_WIDTH - 1 - CONV_WIDTH // 2)],
        dimension_numbers=('NWC', 'WIO', 'NWC'), feature_group_count=C) + conv_b
    ub = u.reshape(B, S, C_BLOCKS, C_BLOCK_W)
    uf = u.astype(jnp.float32)
    h = jnp.zeros_like(uf)
    for d, reverse in ((0, False), (1, True)):
        r = jax.nn.sigmoid((jnp.einsum('bsnc,ncd->bsnd', ub, w_r[d]).reshape(B, S, C) + b_r[d]).astype(jnp.float32))
        i = jax.nn.sigmoid((jnp.einsum('bsnc,ncd->bsnd', ub, w_i[d]).reshape(B, S, C) + b_i[d]).astype(jnp.float32))
        log_a = -RG_C * r * jax.nn.softplus(-lam[d].astype(jnp.float32))
        a = jnp.exp(log_a)
        b = jnp.sqrt(-jnp.expm1(2.0 * log_a)) * (i * uf)
        h = h + _linear_scan(a, b, reverse)
    return (jax.nn.gelu(gate.astype(jnp.float32)) * h).astype(u.dtype)


def _memory_xattn(h, mem, g_mem, w_xq, w_xk, w_xv, w_xo):
    B, S, _ = h.shape
    M = mem.shape[1]
    mn = _rmsnorm(mem, g_mem)
    q = (h @ w_xq).reshape(B, S, X_HEADS, X_HEAD_DIM)
    k = (mn @ w_xk).reshape(B, M, X_HEADS, X_HEAD_DIM)
    v = (mn @ w_xv).reshape(B, M, X_HEADS, X_HEAD_DIM)
    s = jnp.einsum('bqhd,bkhd->bhqk', q, k, preferred_element_type=jnp.float32) * (X_HEAD_DIM ** -0.5)
    p = jax.nn.softmax(s, axis=-1)
    o = jnp.einsum('bhqk,bkhd->bqhd', p.astype(v.dtype), v).reshape(B, S, X_WIDTH)
    return o @ w_xo


def _token_mixing(h, p, l):
    B, S, _ = h.shape
    proj = h @ p['w_in'][l]
    qkv_a, c_q, c_kv, k_rope, u_c, g_c = jnp.split(proj, list(IN_CUTS), axis=-1)
    qkv_a = qkv_a.reshape(B, S, 3, A_HEADS, A_HEAD_DIM)
    y_a = _dilated_mixture(qkv_a[:, :, 0], qkv_a[:, :, 1], qkv_a[:, :, 2], _alibi_slopes(A_HEADS))
    y_b = _mla(c_q, c_kv, k_rope, p['g_q_lat'][l], p['w_q_up'][l], p['g_kv_lat'][l], p['w_kv_up'][l])
    y_c = _rglru_branch(u_c, g_c, p['conv_w'][l], p['conv_b'][l], p['w_rg_r'][l], p['b_rg_r'][l],
                        p['w_rg_i'][l], p['b_rg_i'][l], p['rg_lambda'][l])
    y = jnp.concatenate([_rmsnorm(y_a, p['g_out_a'][l]), _rmsnorm(y_b, p['g_out_b'][l]),
                         _rmsnorm(y_c, p['g_out_c'][l])], axis=-1)
    return y @ p['w_out'][l]


def _trunk(x, mem, p):
    for l in range(DEPTH):
        h = _rmsnorm(x, p['g_ffn1'][l])
        x = x + 0.5 * _swiglu(h, p['w1_gate'][l], p['w1_up'][l], p['w1_down'][l])
        x = x + _token_mixing(_rmsnorm(x, p['g_mix'][l]), p, l)
        x = x + _memory_xattn(_rmsnorm(x, p['g_xattn'][l]), mem, p['g_mem'][l],
                              p['w_xq'][l], p['w_xk'][l], p['w_xv'][l], p['w_xo'][l])
        h = _rmsnorm(x, p['g_ffn2'][l])
        x = x + 0.5 * _swiglu(h, p['w2_gate'][l], p['w2_up'][l], p['w2_down'][l])
    return _rmsnorm(x, p['g_final'])


def setup_inputs(seed: int = 0) -> dict:
    key = jax.random.key(seed)
    ks = iter(jax.random.split(key, 48))
    f32 = jnp.float32

    def nrm(shape, scale):
        return jax.random.normal(next(ks), shape, f32) * scale

    def gain(shape):
        return 1.0 + 0.02 * jax.random.normal(next(ks), shape, f32)

    D, L2 = D_MODEL, DEPTH
    u = jax.random.uniform(next(ks), (L2, 2, C_WIDTH), f32, minval=0.9, maxval=0.999)
    a = u ** (1.0 / RG_C)
    rg_lambda = jnp.log(a) - jnp.log1p(-a)
    return {
        'x_prompt': nrm((BATCH, SEQ, D), 1.0),
        'x_sample': nrm((DEC_BATCH, DEC_SEQ, D), 1.0),
        'mem_prompt': nrm((BATCH, N_MEM, D), 1.0),
        'mem_sample': nrm((DEC_BATCH, N_MEM, D), 1.0),
        'g_ffn1': gain((L2, D)),
        'w1_gate': nrm((L2, D, D_FF), D ** -0.5),
        'w1_up': nrm((L2, D, D_FF), D ** -0.5),
        'w1_down': nrm((L2, D_FF, D), D_FF ** -0.5),
        'g_mix': gain((L2, D)),
        'w_in': nrm((L2, D, IN_WIDTH), D ** -0.5),
        'g_q_lat': gain((L2, Q_LORA)),
        'w_q_up': nrm((L2, Q_LORA, B_HEADS * (QK_NOPE + QK_ROPE)), Q_LORA ** -0.5),
        'g_kv_lat': gain((L2, KV_LORA)),
        'w_kv_up': nrm((L2, KV_LORA, B_HEADS * (QK_NOPE + V_HEAD)), KV_LORA ** -0.5),
        'conv_w': nrm((L2, CONV_WIDTH, C_WIDTH), CONV_WIDTH ** -0.5),
        'conv_b': nrm((L2, C_WIDTH), 0.01),
        'w_rg_r': nrm((L2, 2, C_BLOCKS, C_BLOCK_W, C_BLOCK_W), C_BLOCK_W ** -0.5),
        'b_rg_r': nrm((L2, 2, C_WIDTH), 0.01),
        'w_rg_i': nrm((L2, 2, C_BLOCKS, C_BLOCK_W, C_BLOCK_W), C_BLOCK_W ** -0.5),
        'b_rg_i': nrm((L2, 2, C_WIDTH), 0.01),
        'rg_lambda': rg_lambda,
        'g_out_a': gain((L2, A_WIDTH)),
        'g_out_b': gain((L2, B_WIDTH)),
        'g_out_c': gain((L2, C_WIDTH)),
        'w_out': nrm((L2, MIX_WIDTH, D), MIX_WIDTH ** -0.5),
        'g_xattn': gain((L2, D)),
        'g_mem': gain((L2, D)),
        'w_xq': nrm((L2, D, X_WIDTH), D ** -0.5),
        'w_xk': nrm((L2, D, X_WIDTH), D ** -0.5),
        'w_xv': nrm((L2, D, X_WIDTH), D ** -0.5),
        'w_xo': nrm((L2, X_WIDTH, D), X_WIDTH ** -0.5),
        'g_ffn2': gain((L2, D)),
        'w2_gate': nrm((L2, D, D_FF), D ** -0.5),
        'w2_up': nrm((L2, D, D_FF), D ** -0.5),
        'w2_down': nrm((L2, D_FF, D), D_FF ** -0.5),
        'g_final': gain((D,)),
    }


def reference(x_prompt, x_sample, mem_prompt, mem_sample,
              g_ffn1, w1_gate, w1_up, w1_down,
              g_mix, w_in, g_q_lat, w_q_up, g_kv_lat, w_kv_up,
              conv_w, conv_b, w_rg_r, b_rg_r, w_rg_i, b_rg_i, rg_lambda,
              g_out_a, g_out_b, g_out_c, w_out,
              g_xattn, g_mem, w_xq, w_xk, w_xv, w_xo,
              g_ffn2, w2_gate, w2_up, w2_down, g_final):
    p = dict(g_ffn1=g_ffn1, w1_gate=w1_gate, w1_up=w1_up, w1_down=w1_down,
             g_mix=g_mix, w_in=w_in, g_q_lat=g_q_lat, w_q_up=w_q_up, g_kv_lat=g_kv_lat, w_kv_up=w_kv_up,
             conv_w=conv_w, conv_b=conv_b, w_rg_r=w_rg_r, b_rg_r=b_rg_r, w_rg_i=w_rg_i, b_rg_i=b_rg_i,
             rg_lambda=rg_lambda, g_out_a=g_out_a, g_out_b=g_out_b, g_out_c=g_out_c, w_out=w_out,
             g_xattn=g_xattn, g_mem=g_mem, w_xq=w_xq, w_xk=w_xk, w_xv=w_xv, w_xo=w_xo,
             g_ffn2=g_ffn2, w2_gate=w2_gate, w2_up=w2_up, w2_down=w2_down, g_final=g_final)
    y_prompt = _trunk(x_prompt, mem_prompt, p)
    y_sample = _trunk(x_sample, mem_sample, p)
    return (y_prompt, y_sample)
```

```python
import numpy as np
from contextlib import ExitStack
import concourse.bass as bass
import concourse.mybir as mybir
from concourse.bass_utils import run_bass_kernel_spmd

F32 = mybir.dt.float32
BF16 = mybir.dt.bfloat16
AF = mybir.ActivationFunctionType
ALU = mybir.AluOpType

D = 2048
FF = 5632
TB = 512
NH_A = 6
NH_B = 6
NMEM = 256
EPS = 1e-6
NEG = -30000.0
DILS = (1, 4, 16)
SAME_ENG_SYNC = True


class Sem:
    def __init__(self, h, idx):
        self.h = h
        self.idx = idx
        self.count = 0


class Tk:
    __slots__ = ("w", "r", "sem", "name")

    def __init__(self, name=""):
        self.w = None
        self.r = {}
        self.sem = None
        self.name = name


class Prog:
    ENGS = ("pe", "act", "dve", "pool", "sp")

    def __init__(self, nc, es):
        self.nc = nc
        self.es = es
        self.ops = []
        self.cnt = {e: 0 for e in self.ENGS}
        self.last_c = {e: -1 for e in self.ENGS}
        self.marked = {e: set() for e in self.ENGS}
        self.seen_e = {e: {f: -1 for f in self.ENGS} for e in self.ENGS}
        self.seen_s = {e: {} for e in self.ENGS}
        self.esem = {e: es.enter_context(nc.semaphore("es_" + e)) for e in self.ENGS}
        self.sems = []
        self.free_sems = []
        self.stage_sems = []

    def new_sem(self, persistent=False):
        if not persistent and self.free_sems:
            s = self.free_sems.pop()
        else:
            idx = len(self.sems)
            s = Sem(self.es.enter_context(self.nc.semaphore("ds%d" % idx)), idx)
            self.sems.append(s)
        if not persistent:
            self.stage_sems.append(s)
        return s

    def _wait(self, eng, tok):
        if tok[0] == "e":
            _, f, n = tok
            if f == eng and (eng in ("pe", "sp") or not SAME_ENG_SYNC):
                return
            if self.seen_e[eng][f] >= n:
                return
            self.seen_e[eng][f] = n
            self.marked[f].add(n)
            self.ops.append(("we", eng, f, n))
        else:
            _, s, v = tok
            if self.seen_s[eng].get(s.idx, 0) >= v:
                return
            self.seen_s[eng][s.idx] = v
            self.ops.append(("ws", eng, s, v))

    def op(self, eng, fn, reads=(), writes=(), sem=None):
        for t in reads:
            if t.w is not None:
                self._wait(eng, t.w)
        for t in writes:
            if t.w is not None:
                self._wait(eng, t.w)
            for tok in t.r.values():
                self._wait(eng, tok)
        n = self.cnt[eng]
        self.cnt[eng] += 1
        if sem is None:
            tok = ("e", eng, n)
            key = eng
            self.last_c[eng] = n
        else:
            sem.count += 16
            tok = ("s", sem, sem.count)
            key = ("s", sem.idx)
        self.ops.append(("i", eng, fn, n, sem))
        for t in reads:
            t.r[key] = tok
        for t in writes:
            t.w = tok
            t.r = {}
        return tok

    def barrier(self):
        for e in self.ENGS:
            for f in self.ENGS:
                if f != e and self.last_c[f] >= 0:
                    self._wait(e, ("e", f, self.last_c[f]))
            for s in self.stage_sems:
                if s.count:
                    self._wait(e, ("s", s, s.count))
        self.free_sems.extend(self.stage_sems)
        self.stage_sems = []

    def emit(self):
        nc = self.nc
        eo = {"pe": nc.tensor, "act": nc.scalar, "dve": nc.vector, "pool": nc.gpsimd, "sp": nc.sync}
        rank = {}
        for e in self.ENGS:
            rank[e] = {n: i + 1 for i, n in enumerate(sorted(self.marked[e]))}
        for o in self.ops:
            k = o[0]
            if k == "i":
                _, eng, fn, n, sem = o
                ins = fn(eo[eng])
                if sem is not None:
                    ins.then_inc(sem.h, 16)
                elif n in rank[eng]:
                    ins.then_inc(self.esem[eng], 1)
            elif k == "we":
                _, eng, f, n = o
                eo[eng].wait_ge(self.esem[f], rank[f][n])
            else:
                _, eng, s, v = o
                eo[eng].wait_ge(s.h, v)


class Builder:
    def __init__(self, seqs, depth, dbg=None):
        self.seqs = list(seqs)
        self.depth = depth
        self.NT = sum(seqs)
        self.NM = NMEM * len(seqs)
        self.dbg = dbg or ()
        self.nc = bass.Bass("TRN2", target_bir_lowering=False)
        self.es = ExitStack()

    def dram(self, name, shape, dt, kind="Internal"):
        if name in self.dbg and kind == "Internal":
            kind = "ExternalOutput"
        return self.nc.dram_tensor(name, list(shape), dt, kind=kind).ap()

    def carve(self, n_elems_bf16, name=""):
        off = self.top
        n = (n_elems_bf16 + 15) // 16 * 16
        self.top += n
        assert self.top <= self.ARENA, ("SBUF arena overflow", name, self.top)
        return self.arena[:, off:off + n_elems_bf16], Tk(name)

    def carve32(self, n_f32, name=""):
        ap, tk = self.carve(n_f32 * 2, name)
        return ap.bitcast(F32), tk

    def stage_begin(self):
        self.p.barrier()
        self.top = self.base_top

    def dma(self, q, out, in_, reads, writes, slot):
        if slot.sem is None:
            slot.sem = self.p.new_sem()
        self.p.op(q, lambda e, out=out, in_=in_: e.dma_start(out=out, in_=in_), reads, writes, sem=slot.sem)

    def load(self, out, tk, in_, q="pool", extra_reads=()):
        self.dma(q, out, in_, list(extra_reads), [tk], tk)

    def store(self, out, in_, tk, q="pool"):
        self.dma(q, out, in_, [tk], [], tk)

    def act(self, out, in_, func, reads, writes, **kw):
        self.p.op("act", lambda e: e.activation(out=out, in_=in_, func=func, **kw), reads, writes)

    def mm_group(self, psum_ap, pairs, reads, writes):
        def fn(e, psum_ap=psum_ap, pairs=pairs):
            n = len(pairs)
            ins = None
            for i, (l, r) in enumerate(pairs):
                ins = e.matmul(psum_ap, lhsT=l, rhs=r, start=(i == 0), stop=(i == n - 1))
            return ins
        self.p.op("pe", fn, reads, writes)

    def build(self):
        nc, es = self.nc, self.es
        with es:
            self.p = Prog(nc, es)
            self.ARENA = 105600
            self.arena = es.enter_context(nc.sbuf_tensor("arena", [128, self.ARENA], BF16))[:]
            self.top = 0
            self.banks = []
            for i in range(8):
                ps = es.enter_context(nc.psum_tensor("ps%d" % i, [128, 512], F32))
                self.banks.append((ps[:], Tk("bank%d" % i)))
            self._declare_io()
            self._consts()
            self.base_top = self.top
            self._cast_weights()
            self._transpose_in(self.x_in, self.xT, self.NT)
            self._transpose_in(self.mem_in, self.memT, self.NM)
            for l in range(self.depth):
                self._ffn(l, 1)
                self._inproj(l)
                self._mla(l)
                self._dilated(l)
                self._rglru(l)
                self._outproj(l)
                self._xattn(l)
                self._ffn(l, 2)
            self._final()
            self.p.barrier()
            with nc.allow_non_contiguous_dma(reason="small parameter vectors / strided tiles"):
                self.p.emit()
        return nc

    def _declare_io(self):
        NT, NM, L = self.NT, self.NM, self.depth
        ext = lambda n, s: self.nc.dram_tensor(n, list(s), F32, kind="ExternalInput").ap()
        self.x_in = ext("x_in", (NT, D))
        self.mem_in = ext("mem_in", (NM, D))
        self.rope_in = ext("rope_in", (2, 64, NT))
        self.dbias_in = ext("dbias_in", (128, NH_A * 3 * 4 * 128))
        self.W = {}
        wshapes = dict(w1_gate=(D, FF), w1_up=(D, FF), w1_down=(FF, D), w_in=(D, 4416), w_q_up=(512, 1152),
                       w_kv_up=(512, 1536), w_out=(D, D), w_xq=(D, 512), w_xk=(D, 512), w_xv=(D, 512),
                       w_xo=(512, D), w2_gate=(D, FF), w2_up=(D, FF), w2_down=(FF, D))
        self.wshapes = wshapes
        for k, s in wshapes.items():
            self.W[k] = ext(k, (L,) + s)
        vshapes = dict(g_ffn1=(D,), g_mix=(D,), g_q_lat=(512,), g_kv_lat=(512,), conv_w=(4, 512), conv_b=(512,),
                       w_rg_r=(2, 8, 64, 64), b_rg_r=(2, 512), w_rg_i=(2, 8, 64, 64), b_rg_i=(2, 512),
                       rg_lambda=(2, 512), g_out_a=(768,), g_out_b=(768,), g_out_c=(512,), g_xattn=(D,),
                       g_mem=(D,), g_ffn2=(D,))
        self.V = {}
        for k, s in vshapes.items():
            self.V[k] = ext(k, (L,) + s)
        self.V["g_final"] = ext("g_final", (D,))
        self.y_out = self.nc.dram_tensor("y_out", [NT, D], F32, kind="ExternalOutput").ap()
        self.Wb = {}
        for k, s in wshapes.items():
            self.Wb[k] = [self.dram("wb_%s_%d" % (k, l), s, BF16) for l in range(L)]
        self.xT = self.dram("xT", (D, NT), F32)
        self.memT = self.dram("memT", (D, NM), F32)
        self.qaT = self.dram("qaT", (768, NT), BF16)
        self.kaT = self.dram("kaT", (768, NT), BF16)
        self.va = self.dram("va", (NT, 768), BF16)
        self.qbT = self.dram("qbT", (NH_B * 192, NT), BF16)
        self.kbT = self.dram("kbT", (768, NT), BF16)
        self.krT = self.dram("krT", (64, NT), BF16)
        self.vb = self.dram("vb", (NT, 768), BF16)
        self.uT = self.dram("uT", (512, NT), F32)
        self.gT = self.dram("gT", (512, NT), F32)
        self.yT = self.dram("yT", (D, NT), F32)
        self.kmT = self.dram("kmT", (512, NM), BF16)
        self.vm = self.dram("vm", (NM, 512), BF16)

    def _consts(self):
        p = self.p
        self.ident, self.t_ident = self.carve32(128, "ident")
        self.ones, self.t_ones = self.carve(128, "ones")
        ident, ones = self.ident, self.ones
        p.op("pool", lambda e: e.memset(ident, 0.0), [], [self.t_ident])
        p.op("pool", lambda e: e.affine_select(out=ident, in_=ident, pattern=[[-1, 128]], compare_op=ALU.not_equal,
                                               fill=1.0, base=0, channel_multiplier=1), [self.t_ident], [self.t_ident])
        p.op("pool", lambda e: e.memset(ones, 1.0), [], [self.t_ones])
        self.one_ap, self.one_tk = self.carve32(1, "one")
        one_ap = self.one_ap
        p.op("pool", lambda e: e.memset(one_ap, 1.0), [], [self.one_tk])

    def vec_fm(self, src, n, name):
        nk = n // 128
        ap, tk = self.carve32(nk, name)
        self.load(ap, tk, src.rearrange("(c p) -> p c", p=128), q="sp")
        return ap, tk

    def _cast_weights(self):
        self.wtk = {}
        order = [("w1_gate", "w1_up", "w1_down"), ("w_in", "w_q_up", "w_kv_up"),
                 ("w_out", "w_xq", "w_xk", "w_xv", "w_xo"), ("w2_gate", "w2_up", "w2_down")]
        for l in range(self.depth):
            for grp in order:
                tk = Tk("wg")
                tk.sem = self.p.new_sem(persistent=True)
                for k in grp:
                    r, c = self.wshapes[k]
                    cc = c
                    while cc > 2048:
                        cc //= 2
                    src = self.W[k][l].rearrange("a (b c) -> (a b) c", c=cc)
                    dst = self.Wb[k][l].rearrange("a (b c) -> (a b) c", c=cc)
                    rows = r * (c // cc)
                    step = 4096
                    for r0 in range(0, rows, step):
                        r1 = min(rows, r0 + step)
                        self.dma("pool", dst[r0:r1], src[r0:r1], [], [tk], tk)
                    self.wtk[(k, l)] = tk

    def _transpose_in(self, src, dstT, ntok):
        self.stage_begin()
        p = self.p
        xin = [self.carve32(D, "xin%d" % i) for i in range(2)]
        xts = [self.carve32(16 * TB, "xts%d" % i) for i in range(2)]
        nb = (ntok + TB - 1) // TB
        cnt = 0
        for b in range(nb):
            t0 = b * TB
            tb = min(TB, ntok - t0)
            xt_ap, xt_tk = xts[b % 2]
            xt3 = xt_ap.rearrange("p (c t) -> p c t", c=16)
            for tt in range(tb // 128):
                xi_ap, xi_tk = xin[cnt % 2]
                cnt += 1
                self.load(xi_ap, xi_tk, src[t0 + tt * 128:t0 + (tt + 1) * 128, :])
                for g in range(4):
                    bank, btk = self.banks[(cnt * 4 + g) % 8]

                    def fn(e, bank=bank, xi_ap=xi_ap, g=g):
                        ins = None
                        for j in range(4):
                            c = g * 4 + j
                            ins = e.transpose(bank[:, j * 128:(j + 1) * 128], xi_ap[:, c * 128:(c + 1) * 128], self.ident)
                        return ins
                    p.op("pe", fn, [xi_tk, self.t_ident], [btk])
                    dst = xt3[:, g * 4:(g + 1) * 4, tt * 128:(tt + 1) * 128]
                    srcp = bank.rearrange("p (j t) -> p j t", j=4)
                    if g % 2 == 0:
                        p.op("dve", lambda e, dst=dst, srcp=srcp: e.tensor_copy(out=dst, in_=srcp), [btk], [xt_tk])
                    else:
                        p.op("act", lambda e, dst=dst, srcp=srcp: e.activation(out=dst, in_=srcp, func=AF.Copy), [btk], [xt_tk])
            self.store(dstT.rearrange("(c p) t -> p c t", p=128)[:, :, t0:t0 + tb], xt3[:, :, 0:tb], xt_tk)

    def rms_fm(self, xs3, x_tk, nk0, nk1, g_ap, g_tk, hT3, h_tk, T, bank_i, dn):
        p = self.p
        bank, btk = self.banks[bank_i]
        sq = hT3[:, nk0:nk1, 0:T]
        src = xs3[:, nk0:nk1, 0:T]
        self.act(sq, src, AF.Square, [x_tk], [h_tk])
        pairs = [(self.ones, hT3[:, k, 0:T]) for k in range(nk0, nk1)]
        self.mm_group(bank[:, 0:T], pairs, [h_tk, self.t_ones], [btk])
        s_ap, s_tk = self.rs_buf
        self.act(s_ap[:, 0:T], bank[:, 0:T], AF.Sqrt, [btk, self.eps_tk], [s_tk], bias=self.eps_ap, scale=1.0 / dn)
        p.op("dve", lambda e: e.reciprocal(out=s_ap[:, 0:T], in_=s_ap[:, 0:T]), [s_tk], [s_tk])
        for k in range(nk0, nk1):
            p.op("dve", lambda e, k=k: e.scalar_tensor_tensor(out=hT3[:, k, 0:T], in0=xs3[:, k, 0:T], scalar=g_ap[:, k:k + 1],
                                                             in1=s_ap[:, 0:T], op0=ALU.mult, op1=ALU.mult),
                 [x_tk, s_tk, g_tk], [h_tk])

    def common_bufs(self):
        self.rs_buf = self.carve32(TB, "rstd")
        ea, et = self.carve32(1, "eps")
        self.eps_ap, self.eps_tk = ea, et
        self.p.op("dve", lambda e: e.memset(ea, EPS), [], [et])

    def wslots(self, n, nelem, pool=0):
        if pool == 0:
            self.ws = {}
            self.ws_i = {}
        self.ws[pool] = [self.carve(nelem, "wslot%d_%d" % (pool, i)) for i in range(n)]
        self.ws_i[pool] = 0

    def wload(self, key, l, nk, c0, cw, pool=0):
        ap, tk = self.ws[pool][self.ws_i[pool] % len(self.ws[pool])]
        self.ws_i[pool] += 1
        v = ap[:, 0:nk * cw].rearrange("p (k c) -> p k c", k=nk)
        src = self.Wb[key][l].rearrange("(k p) n -> p k n", p=128)[:, :, c0:c0 + cw]
        self.dma("sp", v, src, [self.wtk[(key, l)]], [tk], tk)
        return v, tk

    def _ffn(self, l, which):
        self.stage_begin()
        p = self.p
        kg, ku, kd, kgain = ("w1_gate", "w1_up", "w1_down", "g_ffn1") if which == 1 else ("w2_gate", "w2_up", "w2_down", "g_ffn2")
        self.common_bufs()
        g_ap, g_tk = self.vec_fm(self.V[kgain][l], D, "gffn")
        xs, x_tk = self.carve32(16 * TB, "xs")
        xs3 = xs.rearrange("p (c t) -> p c t", c=16)
        hT, h_tk = self.carve(16 * TB, "hT")
        hT3 = hT.rearrange("p (c t) -> p c t", c=16)
        actT, a_tk = self.carve(44 * TB, "actT")
        a3 = actT.rearrange("p (c t) -> p c t", c=44)
        sg, sg_tk = self.carve32(TB, "sg")
        sg2, sg2_tk = self.carve32(TB, "sg2")
        sgs = [(sg, sg_tk), (sg2, sg2_tk)]
        CB = 256
        self.wslots(4, 16 * CB, 0)
        self.wslots(2, 44 * CB, 1)
        xTv = self.xT.rearrange("(c p) t -> p c t", p=128)
        nfb = FF // CB
        for b in range(self.NT // TB):
            t0 = b * TB
            self.load(xs3, x_tk, xTv[:, :, t0:t0 + TB])
            self.rms_fm(xs3, x_tk, 0, 16, g_ap, g_tk, hT3, h_tk, TB, 0, D)
            for fb in range(nfb):
                wg, wg_tk = self.wload(kg, l, 16, fb * CB, CB)
                wu, wu_tk = self.wload(ku, l, 16, fb * CB, CB)
                for j in range(CB // 128):
                    f = fb * (CB // 128) + j
                    bi = 1 + 2 * (f % 2)
                    bg, bg_tk = self.banks[bi]
                    bu, bu_tk = self.banks[bi + 1]
                    self.mm_group(bg, [(wg[:, k, j * 128:(j + 1) * 128], hT3[:, k, :]) for k in range(16)], [wg_tk, h_tk], [bg_tk])
                    self.mm_group(bu, [(wu[:, k, j * 128:(j + 1) * 128], hT3[:, k, :]) for k in range(16)], [wu_tk, h_tk], [bu_tk])
                    s_ap, s_tk = sgs[f % 2]
                    self.act(s_ap, bg, AF.Silu, [bg_tk], [s_tk])
                    p.op("dve", lambda e, f=f, s_ap=s_ap, bu=bu: e.tensor_tensor(out=a3[:, f, :], in0=bu, in1=s_ap, op=ALU.mult),
                         [bu_tk, s_tk], [a_tk])
            for db in range(D // CB):
                wd, wd_tk = self.wload(kd, l, 44, db * CB, CB, pool=1)
                for j in range(CB // 128):
                    dc = db * (CB // 128) + j
                    by, by_tk = self.banks[5 + dc % 2]
                    self.mm_group(by, [(wd[:, k, j * 128:(j + 1) * 128], a3[:, k, :]) for k in range(44)], [wd_tk, a_tk], [by_tk])
                    p.op("dve", lambda e, dc=dc, by=by: e.scalar_tensor_tensor(out=xs3[:, dc, :], in0=by, scalar=0.5, in1=xs3[:, dc, :],
                                                                              op0=ALU.mult, op1=ALU.add), [by_tk, x_tk], [x_tk])
            self.store(xTv[:, :, t0:t0 + TB], xs3, x_tk)

    def _final(self):
        self.stage_begin()
        p = self.p
        self.common_bufs()
        g_ap, g_tk = self.vec_fm(self.V["g_final"], D, "gfin")
        xs, x_tk = self.carve32(16 * TB, "xs")
        xs3 = xs.rearrange("p (c t) -> p c t", c=16)
        hT, h_tk = self.carve(16 * TB, "hT")
        hT3 = hT.rearrange("p (c t) -> p c t", c=16)
        yo = [self.carve32(D, "yo%d" % i) for i in range(2)]
        xTv = self.xT.rearrange("(c p) t -> p c t", p=128)
        s_ap, s_tk = self.rs_buf
        cnt = 0
        for b in range(self.NT // TB):
            t0 = b * TB
            self.load(xs3, x_tk, xTv[:, :, t0:t0 + TB])
            self.act(hT3, xs3, AF.Square, [x_tk], [h_tk])
            bank, btk = self.banks[0]
            self.mm_group(bank, [(self.ones, hT3[:, k, :]) for k in range(16)], [h_tk, self.t_ones], [btk])
            self.act(s_ap, bank, AF.Sqrt, [btk, self.eps_tk], [s_tk], bias=self.eps_ap, scale=1.0 / D)
            p.op("dve", lambda e: e.reciprocal(out=s_ap, in_=s_ap), [s_tk], [s_tk])
            for k in range(16):
                p.op("dve", lambda e, k=k: e.scalar_tensor_tensor(out=xs3[:, k, :], in0=xs3[:, k, :], scalar=g_ap[:, k:k + 1],
                                                                 in1=s_ap, op0=ALU.mult, op1=ALU.mult), [x_tk, s_tk, g_tk], [x_tk])
            for tt in range(TB // 128):
                yo_ap, yo_tk = yo[cnt % 2]
                cnt += 1
                for g in range(4):
                    bank, btk = self.banks[1 + (cnt * 4 + g) % 7]

                    def fn(e, bank=bank, g=g, tt=tt):
                        ins = None
                        for j in range(4):
                            c = g * 4 + j
                            ins = e.transpose(bank[:, j * 128:(j + 1) * 128], xs3[:, c, tt * 128:(tt + 1) * 128], self.ident)
                        return ins
                    p.op("pe", fn, [x_tk, self.t_ident], [btk])
                    dst = yo_ap[:, g * 512:(g + 1) * 512]
                    if g % 2 == 0:
                        p.op("dve", lambda e, dst=dst, bank=bank: e.tensor_copy(out=dst, in_=bank), [btk], [yo_tk])
                    else:
                        p.op("act", lambda e, dst=dst, bank=bank: e.activation(out=dst, in_=bank, func=AF.Copy), [btk], [yo_tk])
                self.store(self.y_out[t0 + tt * 128:t0 + (tt + 1) * 128, :], yo_ap, yo_tk)


    def nb(self):
        self._nb = getattr(self, "_nb", 0) + 1
        return self.banks[1 + self._nb % 7]

    def evac(self, out, in_, reads, writes):
        self._ev = getattr(self, "_ev", 0) + 1
        if self._ev % 2:
            self.p.op("dve", lambda e: e.tensor_copy(out=out, in_=in_), reads, writes)
        else:
            self.p.op("act", lambda e: e.activation(out=out, in_=in_, func=AF.Copy), reads, writes)

    def mm1(self, psum_ap, lhsT, rhs, start, stop, reads, writes):
        self.p.op("pe", lambda e: e.matmul(psum_ap, lhsT=lhsT, rhs=rhs, start=start, stop=stop), reads, writes)

    def dve(self, fn, reads, writes):
        self.p.op("dve", fn, reads, writes)

    def seq_ranges(self):
        s0 = 0
        for si, S in enumerate(self.seqs):
            yield si, s0, S
            s0 += S

    def _inproj(self, l):
        self.stage_begin()
        p = self.p
        self.common_bufs()
        g_ap, g_tk = self.vec_fm(self.V["g_mix"][l], D, "gmix")
        gq_ap, gq_tk = self.vec_fm(self.V["g_q_lat"][l], 512, "gq")
        gkv_ap, gkv_tk = self.vec_fm(self.V["g_kv_lat"][l], 512, "gkv")
        xs, x_tk = self.carve32(16 * TB, "xs")
        xs3 = xs.rearrange("p (c t) -> p c t", c=16)
        hT, h_tk = self.carve(16 * TB, "hT")
        hT3 = hT.rearrange("p (c t) -> p c t", c=16)
        wq, wq_tk = self.carve(4 * 1152, "wq")
        wq3 = wq.rearrange("p (k n) -> p k n", k=4)
        self.dma("sp", wq3, self.Wb["w_q_up"][l].rearrange("(k p) n -> p k n", p=128), [self.wtk[("w_q_up", l)]], [wq_tk], wq_tk)
        wkv, wkv_tk = self.carve(4 * 1536, "wkv")
        wkv3 = wkv.rearrange("p (k n) -> p k n", k=4)
        self.dma("sp", wkv3, self.Wb["w_kv_up"][l].rearrange("(k p) n -> p k n", p=128), [self.wtk[("w_kv_up", l)]], [wkv_tk], wkv_tk)
        wkr, wkr_tk = self.carve(16 * 64, "wkr")
        wkr3 = wkr.rearrange("p (k n) -> p k n", k=16)
        self.dma("sp", wkr3, self.Wb["w_in"][l].rearrange("(k p) n -> p k n", p=128)[:, :, 3328:3392], [self.wtk[("w_in", l)]], [wkr_tk], wkr_tk)
        wkrR, wkrR_tk = self.carve(16 * 64, "wkrR")
        wkrR3 = wkrR.rearrange("p (k n) -> p k n", k=16)
        self.dve(lambda e: e.tensor_scalar_mul(out=wkrR3[:, :, 0:32], in0=wkr3[:, :, 32:64], scalar1=-1.0), [wkr_tk], [wkrR_tk])
        self.dve(lambda e: e.tensor_copy(out=wkrR3[:, :, 32:64], in_=wkr3[:, :, 0:32]), [wkr_tk], [wkrR_tk])
        wqR, wqR_tk = self.carve(4 * 6 * 64, "wqR")
        wqR4 = wqR.rearrange("p (k h n) -> p k h n", k=4, h=6)
        wq4 = wq.rearrange("p (k h n) -> p k h n", k=4, h=6)
        for k in range(4):
            self.dve(lambda e, k=k: e.tensor_scalar_mul(out=wqR4[:, k, :, 0:32], in0=wq4[:, k, :, 160:192], scalar1=-1.0), [wq_tk], [wqR_tk])
            self.dve(lambda e, k=k: e.tensor_copy(out=wqR4[:, k, :, 32:64], in_=wq4[:, k, :, 128:160]), [wq_tk], [wqR_tk])
        st6, st6_tk = self.carve(6 * TB, "st6")
        st6v = st6.rearrange("p (c t) -> p c t", c=6)
        st6b, st6b_tk = self.carve(6 * TB, "st6b")
        st6bv = st6b.rearrange("p (c t) -> p c t", c=6)
        vst, vst_tk = self.carve(4 * 768, "vst")
        vst3 = vst.rearrange("p (t c) -> p t c", t=4)
        lat, lat_tk = self.carve32(4 * TB, "lat")
        lat3 = lat.rearrange("p (c t) -> p c t", c=4)
        latn, latn_tk = self.carve(4 * TB, "latn")
        latn3 = latn.rearrange("p (c t) -> p c t", c=4)
        f4, f4_tk = self.carve32(4 * TB, "f4")
        f43 = f4.rearrange("p (c t) -> p c t", c=4)
        cs, cs_tk = self.carve32(2 * TB, "cs")
        cs3 = cs.rearrange("p (c t) -> p c t", c=2)
        r1, r1_tk = self.carve32(TB, "r1")
        r2, r2_tk = self.carve32(TB, "r2")
        qr, qr_tk = self.carve(6 * TB, "qr")
        qr3 = qr.rearrange("p (c t) -> p c t", c=6)
        kr, kr_tk = self.carve(TB, "kr")
        CB = 256
        self.wslots(4, 16 * CB, 0)
        xTv = self.xT.rearrange("(c p) t -> p c t", p=128)

        def rope_combine(bA, bA_tk, bB, bB_tk, out_ap, out_tk):
            self.dve(lambda e: e.tensor_tensor(out=r1[0:64, :], in0=bA[0:64, :], in1=cs3[0:64, 0, :], op=ALU.mult), [bA_tk, cs_tk], [r1_tk])
            self.dve(lambda e: e.tensor_tensor(out=r2[0:64, :], in0=bB[0:64, :], in1=cs3[0:64, 1, :], op=ALU.mult), [bB_tk, cs_tk], [r2_tk])
            self.dve(lambda e: e.tensor_tensor(out=out_ap, in0=r1[0:64, :], in1=r2[0:64, :], op=ALU.add), [r1_tk, r2_tk], [out_tk])

        for b in range(self.NT // TB):
            t0 = b * TB
            self.load(xs3, x_tk, xTv[:, :, t0:t0 + TB])
            self.load(cs3[0:64], cs_tk, self.rope_in.rearrange("c r t -> r c t")[:, :, t0:t0 + TB])
            self.rms_fm(xs3, x_tk, 0, 16, g_ap, g_tk, hT3, h_tk, TB, 0, D)

            def fm_seg(c0, nchunk, dst_fn):
                for blk in range((nchunk + 1) // 2):
                    cw = min(256, (nchunk - blk * 2) * 128)
                    w, w_tk = self.wload("w_in", l, 16, c0 + blk * 256, cw)
                    for j in range(cw // 128):
                        bank, btk = self.nb()
                        self.mm_group(bank, [(w[:, k, j * 128:(j + 1) * 128], hT3[:, k, :]) for k in range(16)], [w_tk, h_tk], [btk])
                        dst_fn(blk * 2 + j, bank, btk)
            fm_seg(0, 6, lambda c, bank, btk: self.evac(st6v[:, c, :], bank, [btk], [st6_tk]))
            self.store(self.qaT.rearrange("(c p) t -> p c t", p=128)[:, :, t0:t0 + TB], st6v, st6_tk)
            fm_seg(768, 6, lambda c, bank, btk: self.evac(st6bv[:, c, :], bank, [btk], [st6b_tk]))
            self.store(self.kaT.rearrange("(c p) t -> p c t", p=128)[:, :, t0:t0 + TB], st6bv, st6b_tk)
            for blk in range(3):
                w, w_tk = self.wload("w_in", l, 16, 1536 + blk * 256, 256)
                for tt in range(4):
                    bank, btk = self.nb()
                    self.mm_group(bank[:, 0:256], [(hT3[:, k, tt * 128:(tt + 1) * 128], w[:, k, :]) for k in range(16)], [w_tk, h_tk], [btk])
                    self.evac(vst3[:, tt, blk * 256:(blk + 1) * 256], bank[:, 0:256], [btk], [vst_tk])
            self.store(self.va[t0:t0 + TB, :].rearrange("(t p) c -> p t c", p=128), vst3, vst_tk)
            fm_seg(2304, 4, lambda c, bank, btk: self.evac(lat3[:, c, :], bank, [btk], [lat_tk]))
            self.rms_fm(lat3, lat_tk, 0, 4, gq_ap, gq_tk, latn3, latn_tk, TB, 0, 512)
            for h in range(NH_B):
                bank, btk = self.nb()
                self.mm_group(bank, [(wq3[:, k, h * 192:h * 192 + 128], latn3[:, k, :]) for k in range(4)], [wq_tk, latn_tk], [btk])
                self.evac(st6v[:, h, :], bank, [btk], [st6_tk])
                bA, bA_tk = self.nb()
                self.mm_group(bA[0:64, :], [(wq3[:, k, h * 192 + 128:h * 192 + 192], latn3[:, k, :]) for k in range(4)], [wq_tk, latn_tk], [bA_tk])
                bB, bB_tk = self.nb()
                self.mm_group(bB[0:64, :], [(wqR4[:, k, h, :], latn3[:, k, :]) for k in range(4)], [wqR_tk, latn_tk], [bB_tk])
                rope_combine(bA, bA_tk, bB, bB_tk, qr3[0:64, h, :], qr_tk)
            qbv = self.qbT.rearrange("(h r) t -> r h t", r=192)
            self.store(qbv[0:128, :, t0:t0 + TB], st6v, st6_tk)
            self.store(qbv[128:192, :, t0:t0 + TB], qr3[0:64], qr_tk)
            fm_seg(2816, 4, lambda c, bank, btk: self.evac(lat3[:, c, :], bank, [btk], [lat_tk]))
            self.rms_fm(lat3, lat_tk, 0, 4, gkv_ap, gkv_tk, latn3, latn_tk, TB, 0, 512)
            for h in range(NH_B):
                bank, btk = self.nb()
                self.mm_group(bank, [(wkv3[:, k, h * 256:h * 256 + 128], latn3[:, k, :]) for k in range(4)], [wkv_tk, latn_tk], [btk])
                self.evac(st6bv[:, h, :], bank, [btk], [st6b_tk])
            self.store(self.kbT.rearrange("(c p) t -> p c t", p=128)[:, :, t0:t0 + TB], st6bv, st6b_tk)
            wkv4 = wkv.rearrange("p (k h c) -> p k h c", k=4, h=6)
            for tt in range(4):
                for half in range(2):
                    bank, btk = self.nb()
                    o3 = bank[:, 0:384].rearrange("p (h c) -> p h c", h=3)
                    self.mm_group(o3, [(latn3[:, k, tt * 128:(tt + 1) * 128], wkv4[:, k, 3 * half:3 * half + 3, 128:256]) for k in range(4)],
                                  [wkv_tk, latn_tk], [btk])
                    self.evac(vst3[:, tt, half * 384:(half + 1) * 384], bank[:, 0:384], [btk], [vst_tk])
            self.store(self.vb[t0:t0 + TB, :].rearrange("(t p) c -> p t c", p=128), vst3, vst_tk)
            bA, bA_tk = self.nb()
            self.mm_group(bA[0:64, :], [(wkr3[:, k, :], hT3[:, k, :]) for k in range(16)], [wkr_tk, h_tk], [bA_tk])
            bB, bB_tk = self.nb()
            self.mm_group(bB[0:64, :], [(wkrR3[:, k, :], hT3[:, k, :]) for k in range(16)], [wkrR_tk, h_tk], [bB_tk])
            rope_combine(bA, bA_tk, bB, bB_tk, kr[0:64, :], kr_tk)
            self.store(self.krT[:, t0:t0 + TB], kr[0:64, :], kr_tk)
            fm_seg(3392, 4, lambda c, bank, btk: self.evac(f43[:, c, :], bank, [btk], [f4_tk]))
            self.store(self.uT.rearrange("(c p) t -> p c t", p=128)[:, :, t0:t0 + TB], f43, f4_tk)
            fm_seg(3904, 4, lambda c, bank, btk: self.evac(lat3[:, c, :], bank, [btk], [lat_tk]))
            self.store(self.gT.rearrange("(c p) t -> p c t", p=128)[:, :, t0:t0 + TB], lat3, lat_tk)

    def _mla(self, l):
        self.stage_begin()
        p = self.p
        scale = float(192 ** -0.5)
        Smax = max(self.seqs)
        KnT, kn_tk = self.carve(Smax, "KnT")
        KrT, kr_tk = self.carve(Smax, "KrT")
        QnT, qn_tk = self.carve(Smax, "QnT")
        QrT, qr_tk = self.carve(Smax, "QrT")
        Vh, v_tk = self.carve(Smax, "Vh")
        Pb = [self.carve(TB, "P%d" % i) for i in range(3)]
        rec, rec_tk = self.carve32(TB, "rec")
        yst = [self.carve32(TB, "yst%d" % i) for i in range(2)]
        qbv = self.qbT.rearrange("(h r) t -> r h t", r=192)
        cnt = 0
        for si, s0, S in self.seq_ranges():
            self.load(KrT[0:64, 0:S], kr_tk, self.krT[:, s0:s0 + S])
            for h in range(NH_B):
                self.load(KnT[:, 0:S], kn_tk, self.kbT[h * 128:(h + 1) * 128, s0:s0 + S])
                self.load(QnT[:, 0:S], qn_tk, qbv[0:128, h, s0:s0 + S])
                self.load(QrT[0:64, 0:S], qr_tk, qbv[128:192, h, s0:s0 + S])
                Vh3 = Vh[:, 0:S].rearrange("p (j c) -> p j c", c=128)
                self.load(Vh3, v_tk, self.vb[s0:s0 + S, h * 128:(h + 1) * 128].rearrange("(j p) c -> p j c", p=128))
                nkt = S // 128
                for qt in range(S // TB):
                    O, O_tk = self.banks[6]
                    Dn, Dn_tk = self.banks[7]
                    q0 = qt * TB
                    for kt in range(nkt):
                        sb, sb_tk = self.banks[kt % 4]
                        self.mm_group(sb, [(KnT[:, kt * 128:(kt + 1) * 128], QnT[:, q0:q0 + TB]),
                                           (KrT[0:64, kt * 128:(kt + 1) * 128], QrT[0:64, q0:q0 + TB])],
                                      [kn_tk, kr_tk, qn_tk, qr_tk], [sb_tk])
                        P, P_tk = Pb[kt % 3]
                        self.act(P, sb, AF.Exp, [sb_tk], [P_tk], scale=scale)
                        self.mm1(O, Vh3[:, kt, :], P, kt == 0, kt == nkt - 1, [v_tk, P_tk], [O_tk])
                        self.mm1(Dn, self.ones, P, kt == 0, kt == nkt - 1, [self.t_ones, P_tk], [Dn_tk])
                    self.dve(lambda e: e.reciprocal(out=rec, in_=Dn), [Dn_tk], [rec_tk])
                    y_ap, y_tk = yst[cnt % 2]
                    cnt += 1
                    self.dve(lambda e, y_ap=y_ap: e.tensor_tensor(out=y_ap, in0=O, in1=rec, op=ALU.mult), [O_tk, rec_tk], [y_tk])
                    self.store(self.yT[768 + h * 128:768 + (h + 1) * 128, s0 + q0:s0 + q0 + TB], y_ap, y_tk)

    def _dilated(self, l):
        self.stage_begin()
        p = self.p
        scale = float(128 ** -0.5)
        Smax = max(self.seqs)
        PADC = 1024
        Kp, kp_tk = self.carve(Smax + 2 * PADC, "Kp")
        Qf, q_tk = self.carve(Smax, "Qf")
        maxtiles = max(dl * (Smax // dl // 128 + 1) for dl in DILS)
        Vc, vc_tk = self.carve(maxtiles * 128, "Vc")
        Vc3 = Vc.rearrange("p (j c) -> p j c", c=128)
        Ua, ua_tk = self.carve32(Smax, "Uacc")
        Za, za_tk = self.carve32(Smax, "Zacc")
        db, db_tk = self.carve32(12 * 128, "dbias")
        db4 = db.rearrange("p (d v q) -> p d v q", d=3, v=4)
        scb, scb_tk = self.carve32(1024, "scb")
        Pb = [self.carve(1024, "dP%d" % i) for i in range(2)]
        gcnt = 0
        for si, s0, S in self.seq_ranges():
            for h in range(NH_A):
                self.p.op("pool", lambda e: e.memset(Kp, 0.0), [], [kp_tk])
                self.load(Kp[:, PADC:PADC + S], kp_tk, self.kaT[h * 128:(h + 1) * 128, s0:s0 + S])
                self.load(Qf[:, 0:S], q_tk, self.qaT[h * 128:(h + 1) * 128, s0:s0 + S])
                self.load(db, db_tk, self.dbias_in[:, h * 1536:(h + 1) * 1536])
                vsrc = self.va[s0:s0 + S, h * 128:(h + 1) * 128]
                for di, Dl in enumerate(DILS):
                    L = S // Dl
                    nq = L // 128
                    self.p.op("pool", lambda e: e.memset(Vc, 0.0), [], [vc_tk])
                    vview = vsrc.rearrange("(j pp d) c -> d pp j c", d=Dl, pp=128)
                    for r in range(Dl):
                        base = r * (nq + 1)
                        self.load(Vc3[64:128, base:base + nq, :], vc_tk, vview[r, 0:64, :, :])
                        self.load(Vc3[0:64, base + 1:base + nq + 1, :], vc_tk, vview[r, 64:128, :, :])
                    pairs = [(r, i) for r in range(Dl) for i in range(nq)]
                    for g0 in range(0, len(pairs), 4):
                        grp = pairs[g0:g0 + 4]
                        ng = len(grp)
                        gcnt += 1
                        sbs = [self.banks[(gcnt % 2) * 2], self.banks[(gcnt % 2) * 2 + 1]]
                        P, P_tk = Pb[gcnt % 2]
                        O, O_tk = self.banks[4 + (gcnt % 2)]
                        Dn, Dn_tk = self.banks[6 + (gcnt % 2)]
                        for g, (r, i) in enumerate(grp):
                            sb, sb_tk = sbs[g // 2]
                            c0 = (g % 2) * 256
                            qs0 = r + Dl * 128 * i
                            qsl = Qf[:, qs0:qs0 + Dl * 127 + 1:Dl]
                            ka0 = PADC + Dl * (128 * i - 64) + r
                            kb0 = PADC + Dl * (128 * i + 64) + r
                            kA = Kp[:, ka0:ka0 + Dl * 127 + 1:Dl]
                            kB = Kp[:, kb0:kb0 + Dl * 127 + 1:Dl]
                            self.mm1(sb[:, c0:c0 + 128], kA, qsl, True, True, [kp_tk, q_tk], [sb_tk])
                            self.mm1(sb[:, c0 + 128:c0 + 256], kB, qsl, True, True, [kp_tk, q_tk], [sb_tk])
                            va_ = 2 if i == 0 else 0
                            vb_ = 3 if i == nq - 1 else 1
                            self.dve(lambda e, sb=sb, c0=c0, g=g, va_=va_, di=di: e.scalar_tensor_tensor(
                                out=scb[:, g * 256:g * 256 + 128], in0=sb[:, c0:c0 + 128], scalar=scale, in1=db4[:, di, va_, :],
                                op0=ALU.mult, op1=ALU.add), [sb_tk, db_tk], [scb_tk])
                            self.dve(lambda e, sb=sb, c0=c0, g=g, vb_=vb_, di=di: e.scalar_tensor_tensor(
                                out=scb[:, g * 256 + 128:g * 256 + 256], in0=sb[:, c0 + 128:c0 + 256], scalar=scale, in1=db4[:, di, vb_, :],
                                op0=ALU.mult, op1=ALU.add), [sb_tk, db_tk], [scb_tk])
                        self.act(P[:, 0:ng * 256], scb[:, 0:ng * 256], AF.Exp, [scb_tk], [P_tk])
                        for g, (r, i) in enumerate(grp):
                            base = r * (nq + 1)
                            self.mm1(O[:, g * 128:(g + 1) * 128], Vc3[:, base + i, :], P[:, g * 256:g * 256 + 128], True, False, [vc_tk, P_tk], [O_tk])
                            self.mm1(O[:, g * 128:(g + 1) * 128], Vc3[:, base + i + 1, :], P[:, g * 256 + 128:g * 256 + 256], False, True, [vc_tk, P_tk], [O_tk])
                            self.mm1(Dn[:, g * 128:(g + 1) * 128], self.ones, P[:, g * 256:g * 256 + 128], True, False, [self.t_ones, P_tk], [Dn_tk])
                            self.mm1(Dn[:, g * 128:(g + 1) * 128], self.ones, P[:, g * 256 + 128:g * 256 + 256], False, True, [self.t_ones, P_tk], [Dn_tk])
                        for g, (r, i) in enumerate(grp):
                            qs0 = r + Dl * 128 * i
                            usl = Ua[:, qs0:qs0 + Dl * 127 + 1:Dl]
                            zsl = Za[:, qs0:qs0 + Dl * 127 + 1:Dl]
                            og = O[:, g * 128:(g + 1) * 128]
                            dg = Dn[:, g * 128:(g + 1) * 128]
                            if di == 0:
                                self.dve(lambda e, usl=usl, og=og: e.tensor_copy(out=usl, in_=og), [O_tk], [ua_tk])
                                self.dve(lambda e, zsl=zsl, dg=dg: e.tensor_copy(out=zsl, in_=dg), [Dn_tk], [za_tk])
                            else:
                                self.dve(lambda e, usl=usl, og=og: e.tensor_tensor(out=usl, in0=og, in1=usl, op=ALU.add), [O_tk, ua_tk], [ua_tk])
                                self.dve(lambda e, zsl=zsl, dg=dg: e.tensor_tensor(out=zsl, in0=dg, in1=zsl, op=ALU.add), [Dn_tk, za_tk], [za_tk])
                self.dve(lambda e, S=S: e.reciprocal(out=Za[:, 0:S], in_=Za[:, 0:S]), [za_tk], [za_tk])
                self.dve(lambda e, S=S: e.tensor_tensor(out=Ua[:, 0:S], in0=Ua[:, 0:S], in1=Za[:, 0:S], op=ALU.mult), [ua_tk, za_tk], [ua_tk])
                self.store(self.yT[h * 128:(h + 1) * 128, s0:s0 + S], Ua[:, 0:S], ua_tk)

    def _rglru(self, l):
        self.stage_begin()
        p = self.p
        Smax = max(self.seqs)
        SEG = 2048
        cw, cw_tk = self.carve32(16, "cw")
        cw3 = cw.rearrange("p (c k) -> p c k", c=4)
        for k in range(4):
            self.load(cw3[:, :, k], cw_tk, self.V["conv_w"][l][k].rearrange("(c p) -> p c", p=128), q="sp")
        cb, cb_tk = self.vec_fm(self.V["conv_b"][l], 512, "cb")
        prm = {}
        for nm in ("b_rg_r", "b_rg_i", "rg_lambda"):
            ap, tk = self.carve32(8, nm)
            ap3 = ap.rearrange("p (d c) -> p d c", d=2)
            for d in range(2):
                self.load(ap3[:, d, :], tk, self.V[nm][l][d].rearrange("(c p) -> p c", p=128), q="sp")
            prm[nm] = (ap3, tk)
        lam3, lam_tk = prm["rg_lambda"]
        c1, c1_tk = self.carve32(8, "c1")
        c2, c2_tk = self.carve32(8, "c2")
        c13 = c1.rearrange("p (d c) -> p d c", d=2)
        c23 = c2.rearrange("p (d c) -> p d c", d=2)
        lamf = lam3.rearrange("p d c -> p (d c)")
        self.act(c1, lamf, AF.Exp, [lam_tk], [c1_tk], scale=-1.0)
        self.p.op("act", lambda e: e.activation(out=c1, in_=c1, func=AF.Ln, bias=self.one_ap), [c1_tk, self.one_tk], [c1_tk])
        self.dve(lambda e: e.tensor_scalar_mul(out=c2, in0=c1, scalar1=-16.0), [c1_tk], [c2_tk])
        self.dve(lambda e: e.tensor_scalar_mul(out=c1, in0=c1, scalar1=-8.0), [c1_tk, c2_tk], [c1_tk])
        wst, wst_tk = self.carve32(128, "wst")
        Wg = {}
        for gi, nm in enumerate(("w_rg_r", "w_rg_i")):
            for d in range(2):
                for c in range(4):
                    wt, wt_tk = self.carve(128, "wg%d%d%d" % (gi, d, c))
                    self.dve(lambda e: e.memset(wst, 0.0), [], [wst_tk])
                    self.load(wst[0:64, 0:64], wst_tk, self.V[nm][l][d, 2 * c], q="sp")
                    self.load(wst[64:128, 64:128], wst_tk, self.V[nm][l][d, 2 * c + 1], q="sp")
                    self.dve(lambda e, wt=wt: e.tensor_copy(out=wt, in_=wst), [wst_tk], [wt_tk])
                    Wg[(gi, d, c)] = (wt, wt_tk)
        ub, ub_tk = self.carve32(SEG + 4, "ubuf")
        ucf, ucf_tk = self.carve32(Smax, "ucf")
        ucb, ucb_tk = self.carve(Smax, "ucb")
        hf, hf_tk = self.carve32(Smax, "hf")
        rb, rb_tk = self.carve32(SEG, "rb")
        ib, ib_tk = self.carve32(SEG, "ib")
        ab, ab_tk = self.carve32(SEG, "ab")
        bb, bb_tk = self.carve32(SEG, "bb")
        hb, hb_tk = self.carve32(SEG, "hb")
        gb, gb_tk = self.carve32(SEG, "gb")
        tb_, tb_tk = self.carve32(SEG, "tb")
        car, car_tk = self.carve32(1, "carry")
        for si, s0, S in self.seq_ranges():
            nseg = S // SEG
            for c in range(4):
                for sg in range(nseg):
                    a0 = sg * SEG
                    lo = max(0, a0 - 2)
                    hi = min(S, a0 + SEG + 1)
                    if lo > a0 - 2 or hi < a0 + SEG + 1:
                        self.dve(lambda e: e.memset(ub, 0.0), [], [ub_tk])
                    self.load(ub[:, lo - (a0 - 2):hi - (a0 - 2)], ub_tk, self.uT[c * 128:(c + 1) * 128, s0 + lo:s0 + hi])
                    dst = ucf[:, a0:a0 + SEG]
                    self.dve(lambda e, dst=dst, c=c: e.tensor_scalar(out=dst, in0=ub[:, 0:SEG], scalar1=cw3[:, c, 0:1], scalar2=cb[:, c:c + 1],
                                                                     op0=ALU.mult, op1=ALU.add), [ub_tk, cw_tk, cb_tk], [ucf_tk])
                    for k in range(1, 4):
                        self.dve(lambda e, dst=dst, c=c, k=k: e.scalar_tensor_tensor(out=dst, in0=ub[:, k:k + SEG], scalar=cw3[:, c, k:k + 1], in1=dst,
                                                                                   op0=ALU.mult, op1=ALU.add), [ub_tk, cw_tk, ucf_tk], [ucf_tk])
                    self.act(ucb[:, a0:a0 + SEG], dst, AF.Copy, [ucf_tk], [ucb_tk])
                for d in range(2):
                    order = range(nseg) if d == 0 else range(nseg - 1, -1, -1)
                    first = True
                    for sg in order:
                        a0 = sg * SEG
                        for t in range(SEG // TB):
                            for gi, (dstb, dst_tk, bnm) in enumerate(((rb, rb_tk, "b_rg_r"), (ib, ib_tk, "b_rg_i"))):
                                wt, wt_tk = Wg[(gi, d, c)]
                                bank, btk = self.nb()
                                self.mm1(bank, wt, ucb[:, a0 + t * TB:a0 + (t + 1) * TB], True, True, [wt_tk, ucb_tk], [btk])
                                bap, b_tk = prm[bnm]
                                self.act(dstb[:, t * TB:(t + 1) * TB], bank, AF.Sigmoid, [btk, b_tk], [dst_tk], bias=bap[:, d, c:c + 1])
                        self.act(ab, rb, AF.Exp, [rb_tk, c1_tk], [ab_tk], scale=c13[:, d, c:c + 1])
                        self.act(tb_, rb, AF.Exp, [rb_tk, c2_tk], [tb_tk], scale=c23[:, d, c:c + 1])
                        self.dve(lambda e: e.tensor_scalar(out=tb_, in0=tb_, scalar1=-1.0, scalar2=1.0, op0=ALU.mult, op1=ALU.add), [tb_tk], [tb_tk])
                        self.dve(lambda e: e.tensor_scalar_max(out=tb_, in0=tb_, scalar1=0.0), [tb_tk], [tb_tk])
                        self.act(tb_, tb_, AF.Sqrt, [tb_tk], [tb_tk])
                        self.dve(lambda e, a0=a0: e.tensor_tensor(out=bb, in0=ib, in1=ucf[:, a0:a0 + SEG], op=ALU.mult), [ib_tk, ucf_tk], [bb_tk])
                        self.dve(lambda e: e.tensor_tensor(out=bb, in0=bb, in1=tb_, op=ALU.mult), [bb_tk, tb_tk], [bb_tk])
                        if d == 0:
                            out = hf[:, a0:a0 + SEG]
                            init = 0.0 if first else hf[:, a0 - 1:a0]
                            self.dve(lambda e, out=out, init=init: e.tensor_tensor_scan(out=out, data0=ab, data1=bb, initial=init, op0=ALU.mult, op1=ALU.add),
                                     [ab_tk, bb_tk, hf_tk], [hf_tk])
                        else:
                            init = 0.0 if first else car
                            self.dve(lambda e, init=init: e.tensor_tensor_scan(out=hb[:, ::-1], data0=ab[:, ::-1], data1=bb[:, ::-1], initial=init,
                                                                               op0=ALU.mult, op1=ALU.add), [ab_tk, bb_tk, car_tk], [hb_tk])
                            self.dve(lambda e: e.tensor_copy(out=car, in_=hb[:, 0:1]), [hb_tk], [car_tk])
                            self.load(gb, gb_tk, self.gT[c * 128:(c + 1) * 128, s0 + a0:s0 + a0 + SEG])
                            self.dve(lambda e, a0=a0: e.tensor_tensor(out=hb, in0=hb, in1=hf[:, a0:a0 + SEG], op=ALU.add), [hb_tk, hf_tk], [hb_tk])
                            self.dve(lambda e: e.tensor_tensor(out=tb_, in0=gb, in1=gb, op=ALU.mult), [gb_tk], [tb_tk])
                            self.dve(lambda e: e.tensor_scalar(out=tb_, in0=tb_, scalar1=0.044715, scalar2=1.0, op0=ALU.mult, op1=ALU.add), [tb_tk], [tb_tk])
                            self.dve(lambda e: e.tensor_tensor(out=tb_, in0=tb_, in1=gb, op=ALU.mult), [tb_tk, gb_tk], [tb_tk])
                            self.act(tb_, tb_, AF.Sigmoid, [tb_tk], [tb_tk], scale=float(2.0 * np.sqrt(2.0 / np.pi)))
                            self.dve(lambda e: e.tensor_tensor(out=tb_, in0=tb_, in1=gb, op=ALU.mult), [tb_tk, gb_tk], [tb_tk])
                            self.dve(lambda e: e.tensor_tensor(out=hb, in0=hb, in1=tb_, op=ALU.mult), [hb_tk, tb_tk], [hb_tk])
                            self.store(self.yT[1536 + c * 128:1536 + (c + 1) * 128, s0 + a0:s0 + a0 + SEG], hb, hb_tk)
                        first = False

    def _outproj(self, l):
        self.stage_begin()
        p = self.p
        self.common_bufs()
        go, go_tk = self.carve32(16, "gout")
        self.load(go[:, 0:6], go_tk, self.V["g_out_a"][l].rearrange("(c p) -> p c", p=128), q="sp")
        self.load(go[:, 6:12], go_tk, self.V["g_out_b"][l].rearrange("(c p) -> p c", p=128), q="sp")
        self.load(go[:, 12:16], go_tk, self.V["g_out_c"][l].rearrange("(c p) -> p c", p=128), q="sp")
        xs, x_tk = self.carve32(16 * TB, "xs")
        xs3 = xs.rearrange("p (c t) -> p c t", c=16)
        ys, y_tk = self.carve32(16 * TB, "ys")
        ys3 = ys.rearrange("p (c t) -> p c t", c=16)
        hT, h_tk = self.carve(16 * TB, "hT")
        hT3 = hT.rearrange("p (c t) -> p c t", c=16)
        CB = 256
        self.wslots(4, 16 * CB, 0)
        xTv = self.xT.rearrange("(c p) t -> p c t", p=128)
        yTv = self.yT.rearrange("(c p) t -> p c t", p=128)
        for b in range(self.NT // TB):
            t0 = b * TB
            self.load(ys3, y_tk, yTv[:, :, t0:t0 + TB])
            self.load(xs3, x_tk, xTv[:, :, t0:t0 + TB])
            self.rms_fm(ys3, y_tk, 0, 6, go, go_tk, hT3, h_tk, TB, 0, 768)
            self.rms_fm(ys3, y_tk, 6, 12, go, go_tk, hT3, h_tk, TB, 0, 768)
            self.rms_fm(ys3, y_tk, 12, 16, go, go_tk, hT3, h_tk, TB, 0, 512)
            for blk in range(D // CB):
                w, w_tk = self.wload("w_out", l, 16, blk * CB, CB)
                for j in range(CB // 128):
                    dc = blk * (CB // 128) + j
                    bank, btk = self.nb()
                    self.mm_group(bank, [(w[:, k, j * 128:(j + 1) * 128], hT3[:, k, :]) for k in range(16)], [w_tk, h_tk], [btk])
                    self.dve(lambda e, dc=dc, bank=bank: e.tensor_tensor(out=xs3[:, dc, :], in0=bank, in1=xs3[:, dc, :], op=ALU.add), [btk, x_tk], [x_tk])
            self.store(xTv[:, :, t0:t0 + TB], xs3, x_tk)

    def _xattn(self, l):
        self.stage_begin()
        p = self.p
        scale = float(128 ** -0.5)
        self.common_bufs()
        gx, gx_tk = self.vec_fm(self.V["g_xattn"][l], D, "gx")
        gm, gm_tk = self.vec_fm(self.V["g_mem"][l], D, "gm")
        xs, x_tk = self.carve32(16 * TB, "xs")
        xs3 = xs.rearrange("p (c t) -> p c t", c=16)
        hT, h_tk = self.carve(16 * TB, "hT")
        hT3 = hT.rearrange("p (c t) -> p c t", c=16)
        nseq = len(self.seqs)
        km, km_tk = self.carve(4 * self.NM, "km")
        km3 = km.rearrange("p (h t) -> p h t", h=4)
        vmb, vm_tk = self.carve(nseq * 2 * 512, "vm")
        vm3 = vmb.rearrange("p (j c) -> p j c", c=512)
        q, q_tk = self.carve(4 * TB, "q")
        q3 = q.rearrange("p (h t) -> p h t", h=4)
        o, o_tk = self.carve(4 * TB, "o")
        o3 = o.rearrange("p (h t) -> p h t", h=4)
        Pb = [self.carve(TB, "xP%d" % i) for i in range(2)]
        rec, rec_tk = self.carve32(TB, "rec")
        self.wslots(3, 16 * 512, 0)
        xTv = self.xT.rearrange("(c p) t -> p c t", p=128)
        mTv = self.memT.rearrange("(c p) t -> p c t", p=128)
        wk, wk_tk = self.wload("w_xk", l, 16, 0, 512)
        wv, wv_tk = self.wload("w_xv", l, 16, 0, 512)
        for si in range(nseq):
            self.load(xs3[:, :, 0:256], x_tk, mTv[:, :, si * 256:(si + 1) * 256])
            self.rms_fm(xs3, x_tk, 0, 16, gm, gm_tk, hT3, h_tk, 256, 0, D)
            for h in range(4):
                bank, btk = self.nb()
                self.mm_group(bank[:, 0:256], [(wk[:, k, h * 128:(h + 1) * 128], hT3[:, k, 0:256]) for k in range(16)], [wk_tk, h_tk], [btk])
                self.evac(km3[:, h, si * 256:(si + 1) * 256], bank[:, 0:256], [btk], [km_tk])
            for tt in range(2):
                bank, btk = self.nb()
                self.mm_group(bank, [(hT3[:, k, tt * 128:(tt + 1) * 128], wv[:, k, :]) for k in range(16)], [wv_tk, h_tk], [btk])
                self.evac(vm3[:, si * 2 + tt, :], bank, [btk], [vm_tk])
        for si, s0, S in self.seq_ranges():
            for b in range(S // TB):
                t0 = s0 + b * TB
                self.load(xs3, x_tk, xTv[:, :, t0:t0 + TB])
                self.rms_fm(xs3, x_tk, 0, 16, gx, gx_tk, hT3, h_tk, TB, 0, D)
                wq, wq_tk = self.wload("w_xq", l, 16, 0, 512)
                for h in range(4):
                    bank, btk = self.nb()
                    self.mm_group(bank, [(wq[:, k, h * 128:(h + 1) * 128], hT3[:, k, :]) for k in range(16)], [wq_tk, h_tk], [btk])
                    self.evac(q3[:, h, :], bank, [btk], [q_tk])
                for h in range(4):
                    O, O_tk = self.nb()
                    Dn, Dn_tk = self.nb()
                    for kt in range(2):
                        sb, sb_tk = self.nb()
                        self.mm1(sb, km3[:, h, si * 256 + kt * 128:si * 256 + (kt + 1) * 128], q3[:, h, :], True, True, [km_tk, q_tk], [sb_tk])
                        P, P_tk = Pb[kt]
                        self.act(P, sb, AF.Exp, [sb_tk], [P_tk], scale=scale)
                        self.mm1(O, vm3[:, si * 2 + kt, h * 128:(h + 1) * 128], P, kt == 0, kt == 1, [vm_tk, P_tk], [O_tk])
                        self.mm1(Dn, self.ones, P, kt == 0, kt == 1, [self.t_ones, P_tk], [Dn_tk])
                    self.dve(lambda e, Dn=Dn: e.reciprocal(out=rec, in_=Dn), [Dn_tk], [rec_tk])
                    self.dve(lambda e, O=O, h=h: e.tensor_tensor(out=o3[:, h, :], in0=O, in1=rec, op=ALU.mult), [O_tk, rec_tk], [o_tk])
                wo, wo_tk = self.wload("w_xo", l, 4, 0, D)
                for dc in range(16):
                    bank, btk = self.nb()
                    self.mm_group(bank, [(wo[:, hh, dc * 128:(dc + 1) * 128], o3[:, hh, :]) for hh in range(4)], [wo_tk, o_tk], [btk])
                    self.dve(lambda e, dc=dc, bank=bank: e.tensor_tensor(out=xs3[:, dc, :], in0=bank, in1=xs3[:, dc, :], op=ALU.add), [btk, x_tk], [x_tk])
                self.store(xTv[:, :, t0:t0 + TB], xs3, x_tk)


def rope_tab(seqs):
    tabs = []
    for S in seqs:
        inv = (10000.0 ** (-np.arange(0, 64, 2, dtype=np.float32) / 64)).astype(np.float32)
        ang = np.arange(S, dtype=np.float32)[:, None] * inv[None, :]
        c = np.cos(ang).T.astype(np.float32)
        s_ = np.sin(ang).T.astype(np.float32)
        tabs.append(np.stack([np.concatenate([c, c], 0), np.concatenate([s_, s_], 0)]))
    return np.ascontiguousarray(np.concatenate(tabs, axis=2))


def dbias_tab():
    slopes = 2.0 ** (-8.0 * np.arange(1, NH_A + 1, dtype=np.float32) / NH_A)
    kk = np.arange(128)[:, None]
    qq = np.arange(128)[None, :]
    out = np.zeros((128, NH_A, 3, 4, 128), np.float32)
    for h in range(NH_A):
        for di, dil in enumerate(DILS):
            for var in range(4):
                dm = kk - qq - 64 if var in (0, 2) else kk - qq + 64
                valid = np.abs(dm) <= 64
                if var == 2:
                    valid = valid & (kk >= 64)
                if var == 3:
                    valid = valid & (kk < 64)
                b = -slopes[h] * dil * np.abs(dm).astype(np.float32)
                out[:, h, di, var, :] = np.where(valid, b, NEG)
    return np.ascontiguousarray(out.reshape(128, -1))


WKEYS = ("w1_gate", "w1_up", "w1_down", "w_in", "w_q_up", "w_kv_up", "w_out", "w_xq", "w_xk", "w_xv", "w_xo",
         "w2_gate", "w2_up", "w2_down")
VKEYS = ("g_ffn1", "g_mix", "g_q_lat", "g_kv_lat", "conv_w", "conv_b", "w_rg_r", "b_rg_r", "w_rg_i", "b_rg_i",
         "rg_lambda", "g_out_a", "g_out_b", "g_out_c", "g_xattn", "g_mem", "g_ffn2", "g_final")


def make_inmap(x_list, mem_list, params, rope, dbias):
    d = {"x_in": np.ascontiguousarray(np.concatenate(x_list, 0)),
         "mem_in": np.ascontiguousarray(np.concatenate(mem_list, 0)),
         "rope_in": rope, "dbias_in": dbias}
    for k in WKEYS + VKEYS:
        d[k] = np.ascontiguousarray(params[k])
    return d


SEQS = (8192, 2048)
DEPTH = 2
_CACHE = {}


def kernel(**inputs):
    inputs = {k: np.asarray(v) for k, v in inputs.items()}
    xp, xsm = inputs["x_prompt"], inputs["x_sample"]
    mp, msm = inputs["mem_prompt"], inputs["mem_sample"]
    seqs = (xp.shape[1], xsm.shape[1])
    depth = inputs["w_in"].shape[0]
    key = (seqs, depth)
    if key not in _CACHE:
        _CACHE[key] = Builder(seqs, depth).build()
    nc = _CACHE[key]
    rope = rope_tab(seqs)
    dbias = dbias_tab()
    nb = xp.shape[0]
    in_maps = []
    for c in range(8):
        i = c % nb
        in_maps.append(make_inmap([xp[i], xsm[i]], [mp[i], msm[i]], inputs, rope, dbias))
    res = run_bass_kernel_spmd(nc, in_maps, core_ids=list(range(8)))
    yp = np.stack([np.asarray(res.results[i]["y_out"])[:seqs[0]] for i in range(nb)]).astype(np.float32)
    ys = np.stack([np.asarray(res.results[i]["y_out"])[seqs[0]:] for i in range(nb)]).astype(np.float32)
    return (yp, ys)
```
